# Optimizing a Trainium2 kernel written in Bass

```python
import jax, jax.numpy as jnp
from jax import lax
import numpy as np

D_MODEL = 2048
BATCH = 2
SEQ = 4096
DEPTH = 4
DEC_BATCH = 32
DEC_SEQ = 8
PAST_LEN = 16384
PAGE_SIZE = 128

N_HEADS = D_MODEL // 128
HEAD_DIM = 64
N_KV = N_HEADS // 4
GQA_G = N_HEADS // N_KV
WINDOW = 128
ATT_BLOCK = WINDOW
BR_W = N_HEADS * HEAD_DIM
CONV_W = BR_W
CONV_K = 31
POOL_GROUPS = 4
POOL_WINDOWS = (2, 4, 8, 16)
POOL_W = BR_W
POOL_G = POOL_W // POOL_GROUPS
POOL_PAD = max(POOL_WINDOWS) - 1
RET_HEADS = 8
RET_DK = 64
RET_DV = BR_W // RET_HEADS
RET_QK_W = RET_HEADS * RET_DK
RET_CHUNK = 128
N_BR = 4
D_FF = 4 * D_MODEL
EPS = 1e-6
Q_W = N_HEADS * HEAD_DIM
KV_W = N_KV * HEAD_DIM
SPLIT_SIZES = (Q_W, KV_W, KV_W, 2 * CONV_W, POOL_W, RET_QK_W, RET_QK_W, BR_W, BR_W, N_BR * D_MODEL)
N_IN = sum(SPLIT_SIZES)

kernel_name = 'hybrid_gated_swa_conv_pool_retention_step'


def rms_norm(x, g):
    xf = x.astype(jnp.float32)
    y = xf * lax.rsqrt(jnp.mean(xf * xf, axis=-1, keepdims=True) + EPS)
    return (y * g.astype(jnp.float32)).astype(x.dtype)


def layer_norm(x, g, b):
    xf = x.astype(jnp.float32)
    xc = xf - jnp.mean(xf, axis=-1, keepdims=True)
    var = jnp.mean(xc * xc, axis=-1, keepdims=True)
    return (xc * lax.rsqrt(var + EPS) * g.astype(jnp.float32) + b.astype(jnp.float32)).astype(x.dtype)


def alibi_slopes():
    h = jnp.arange(1, N_HEADS + 1, dtype=jnp.float32)
    return jnp.exp2(-8.0 * h / N_HEADS).reshape(N_KV, GQA_G)


def sink_attention(q, k, v, dist, valid, sinks):
    s = jnp.einsum('bnqkgd,bnskd->bnkgqs', q.astype(jnp.float32), k.astype(jnp.float32)) * (HEAD_DIM ** -0.5)
    s = s - alibi_slopes()[:, :, None, None] * dist.astype(jnp.float32)
    s = jnp.where(valid[None, :, None, None], s, -jnp.inf)
    sink = jnp.broadcast_to(sinks.astype(jnp.float32).reshape(N_KV, GQA_G, 1, 1), s.shape[:-1] + (1,))
    p = jax.nn.softmax(jnp.concatenate([s, sink], axis=-1), axis=-1)[..., :-1]
    o = jnp.einsum('bnkgqs,bnskd->bnqkgd', p, v.astype(jnp.float32))
    return o.astype(q.dtype)


def attn_prompt(q, k, v, sinks):
    B, T = q.shape[:2]
    NB = T // ATT_BLOCK
    qb = q.reshape(B, NB, ATT_BLOCK, N_KV, GQA_G, HEAD_DIM)

    def with_prev(z):
        zb = z.reshape(B, NB, ATT_BLOCK, N_KV, HEAD_DIM)
        prev = jnp.concatenate([jnp.zeros_like(zb[:, :1]), zb[:, :-1]], axis=1)
        return jnp.concatenate([prev, zb], axis=2)

    i = jnp.arange(ATT_BLOCK)[:, None]
    j = jnp.arange(2 * ATT_BLOCK)[None, :]
    dist = ATT_BLOCK + i - j
    blk = jnp.arange(NB)[:, None, None]
    valid = ((dist >= 0) & (dist <= WINDOW))[None] & ((blk > 0) | (j[None] >= ATT_BLOCK))
    o = sink_attention(qb, with_prev(k), with_prev(v), dist, valid, sinks)
    return o.reshape(B, T, Q_W)


def attn_sample(q, k, v, k_buf, v_buf, sinks):
    B, T = q.shape[:2]
    kk = jnp.concatenate([k_buf.astype(k.dtype), k], axis=1)
    vv = jnp.concatenate([v_buf.astype(v.dtype), v], axis=1)
    i = jnp.arange(T)[:, None]
    j = jnp.arange(WINDOW + T)[None, :]
    dist = WINDOW + i - j
    valid = ((dist >= 0) & (dist <= WINDOW))[None]
    o = sink_attention(q.reshape(B, 1, T, N_KV, GQA_G, HEAD_DIM), kk[:, None], vv[:, None], dist, valid, sinks)
    return o.reshape(B, T, Q_W), kk[:, -WINDOW:], vv[:, -WINDOW:]


def conv_module(a, prefix, w_dw, b_dw, g_ln, b_ln):
    u = a[..., :CONV_W] * jax.nn.sigmoid(a[..., CONV_W:])
    ext = jnp.concatenate([prefix.astype(u.dtype), u], axis=1)
    y = lax.conv_general_dilated(ext, w_dw[:, None, :].astype(ext.dtype), window_strides=(1,), padding='VALID',
                                 dimension_numbers=('NWC', 'WIO', 'NWC'), feature_group_count=CONV_W)
    y = jax.nn.silu(layer_norm(y + b_dw, g_ln, b_ln))
    return y, ext[:, -(CONV_K - 1):]


def pool_mixer(u, prefix, pos0, w_pool, s_pool):
    B, T = u.shape[:2]
    ext_in = jnp.concatenate([prefix.astype(u.dtype), u], axis=1)
    ext = ext_in.astype(jnp.float32)
    cs = jnp.concatenate([jnp.zeros((B, 1, POOL_W), jnp.float32), lax.cumsum(ext, axis=1)], axis=1)
    pos = jnp.arange(T) + pos0
    outs = []
    for g, w in enumerate(POOL_WINDOWS):
        sl = slice(g * POOL_G, (g + 1) * POOL_G)
        wsum = cs[:, POOL_PAD + 1:POOL_PAD + 1 + T, sl] - cs[:, POOL_PAD + 1 - w:POOL_PAD + 1 - w + T, sl]
        cnt = jnp.minimum(pos + 1, w).astype(jnp.float32)
        outs.append(wsum / cnt[None, :, None])
    z = (jnp.concatenate(outs, axis=-1) - ext[:, POOL_PAD:]).reshape(B, T, POOL_GROUPS, POOL_G)
    z = jnp.einsum('btgc,gcd->btgd', z, w_pool.astype(jnp.float32)).reshape(B, T, POOL_W)
    z = z * s_pool.astype(jnp.float32)
    return z.astype(u.dtype), ext_in[:, -POOL_PAD:]


def retention(q, k, v, s0):
    B, T = q.shape[:2]
    C = RET_CHUNK if T % RET_CHUNK == 0 else T
    N = T // C
    log_g = jnp.log1p(-jnp.exp2(-5.0 - jnp.arange(RET_HEADS, dtype=jnp.float32)))
    qf = q.astype(jnp.float32).reshape(B, N, C, RET_HEADS, RET_DK)
    kf = k.astype(jnp.float32).reshape(B, N, C, RET_HEADS, RET_DK) * (RET_DK ** -0.5)
    vf = v.astype(jnp.float32).reshape(B, N, C, RET_HEADS, RET_DV)
    i = jnp.arange(C, dtype=jnp.float32)
    diff = i[:, None] - i[None, :]
    decay = jnp.where(diff >= 0, jnp.exp(log_g[:, None, None] * jnp.maximum(diff, 0.0)), 0.0)
    scores = jnp.einsum('bnihd,bnjhd->bnhij', qf, kf) * decay
    inner = jnp.einsum('bnhij,bnjhv->bnihv', scores, vf)
    k_dec = kf * jnp.exp(log_g[None, :] * (C - 1 - i)[:, None])[:, :, None]
    kv = jnp.einsum('bnjhd,bnjhv->bnhdv', k_dec, vf)
    g_chunk = jnp.exp(log_g * C)[None, :, None, None]

    def step(S, kv_n):
        return g_chunk * S + kv_n, S

    s_fin, s_prev = lax.scan(step, s0.astype(jnp.float32), jnp.moveaxis(kv, 1, 0))
    s_prev = jnp.moveaxis(s_prev, 0, 1)
    cross = jnp.einsum('bnihd,bnhdv->bnihv', qf, s_prev) * jnp.exp(log_g[None, :] * (i + 1)[:, None])[:, :, None]
    return (inner + cross).reshape(B, T, RET_HEADS, RET_DV), s_fin


def head_group_norm(o, g):
    B, T = o.shape[:2]
    oc = o - jnp.mean(o, axis=-1, keepdims=True)
    var = jnp.mean(oc * oc, axis=-1, keepdims=True)
    return (oc * lax.rsqrt(var + EPS)).reshape(B, T, BR_W) * g.astype(jnp.float32)


def trunk_layer(x, c, p, cache):
    (w_ada, b_ada, g_norm1, g_norm2, w_in, g_qnorm, g_knorm, attn_sinks, w_dw, b_dw, g_conv_ln, b_conv_ln,
     w_pool, s_pool, g_ret_norm, w_br, w_out, w_mlp1, w_mlp2) = p
    B, T, _ = x.shape
    ada = (jax.nn.silu(c) @ w_ada + b_ada)[:, None, :]
    sh1, sc1, gt1, sh2, sc2, gt2 = jnp.split(ada, 6, axis=-1)
    h = rms_norm(x, g_norm1) * (1 + sc1) + sh1
    z = h @ w_in
    pts = [int(s) for s in np.cumsum(SPLIT_SIZES)[:-1]]
    q, k, v, a_conv, u_pool, rq, rk, rv, rg, gpre = jnp.split(z, pts, axis=-1)
    q = rms_norm(q.reshape(B, T, N_HEADS, HEAD_DIM), g_qnorm)
    k = rms_norm(k.reshape(B, T, N_KV, HEAD_DIM), g_knorm)
    v = v.reshape(B, T, N_KV, HEAD_DIM)
    if cache is None:
        y_att = attn_prompt(q, k, v, attn_sinks)
        k_new, v_new = k[:, -WINDOW:], v[:, -WINDOW:]
        conv_prefix = jnp.zeros((B, CONV_K - 1, CONV_W), z.dtype)
        pool_prefix = jnp.zeros((B, POOL_PAD, POOL_W), z.dtype)
        s0 = jnp.zeros((B, RET_HEADS, RET_DK, RET_DV), jnp.float32)
        pos0 = 0
    else:
        k_buf, v_buf, conv_prefix, pool_prefix, s0 = cache
        y_att, k_new, v_new = attn_sample(q, k, v, k_buf, v_buf, attn_sinks)
        pos0 = PAST_LEN
    y_conv, conv_new = conv_module(a_conv, conv_prefix, w_dw, b_dw, g_conv_ln, b_conv_ln)
    y_pool, pool_new = pool_mixer(u_pool, pool_prefix, pos0, w_pool, s_pool)
    o_ret, s_new = retention(rq.reshape(B, T, RET_HEADS, RET_DK), rk.reshape(B, T, RET_HEADS, RET_DK),
                             rv.reshape(B, T, RET_HEADS, RET_DV), s0)
    y_ret = (head_group_norm(o_ret, g_ret_norm) * jax.nn.silu(rg.astype(jnp.float32))).astype(x.dtype)
    branches = jnp.stack([y_att, y_conv, y_pool, y_ret], axis=2)
    proj = jnp.einsum('btrc,rcd->btrd', branches, w_br)
    gates = jax.nn.sigmoid(gpre.astype(jnp.float32)).reshape(B, T, N_BR, D_MODEL)
    merged = jnp.einsum('btrd,btrd->btd', gates, proj.astype(jnp.float32)).astype(x.dtype)
    x = x + gt1 * (merged @ w_out)
    h2 = rms_norm(x, g_norm2) * (1 + sc2) + sh2
    x = x + gt2 * (jnp.square(jax.nn.relu(h2 @ w_mlp1)) @ w_mlp2)
    return x, (k_new, v_new, conv_new, pool_new, s_new)


def setup_inputs(seed: int = 0) -> dict:
    key = jax.random.key(seed)
    ks = jax.random.split(key, 32)
    f32 = jnp.float32

    def nrm(k, shape, s):
        return jax.random.normal(k, shape, f32) * s

    return {
        'x_prompt': nrm(ks[0], (BATCH, SEQ, D_MODEL), 1.0),
        'x_sample': nrm(ks[1], (DEC_BATCH, DEC_SEQ, D_MODEL), 1.0),
        'c_prompt': nrm(ks[2], (BATCH, D_MODEL), 1.0),
        'c_sample': nrm(ks[3], (DEC_BATCH, D_MODEL), 1.0),
        'cache_attn_k': nrm(ks[4], (DEPTH, DEC_BATCH, WINDOW, N_KV, HEAD_DIM), 1.0),
        'cache_attn_v': nrm(ks[5], (DEPTH, DEC_BATCH, WINDOW, N_KV, HEAD_DIM), 1.0),
        'state_conv': nrm(ks[6], (DEPTH, DEC_BATCH, CONV_K - 1, CONV_W), 0.5),
        'state_pool': nrm(ks[7], (DEPTH, DEC_BATCH, POOL_PAD, POOL_W), 1.0),
        'state_ret': nrm(ks[8], (DEPTH, DEC_BATCH, RET_HEADS, RET_DK, RET_DV), 1.0),
        'w_ada': nrm(ks[9], (DEPTH, D_MODEL, 6 * D_MODEL), 0.2 * D_MODEL ** -0.5),
        'b_ada': nrm(ks[10], (DEPTH, 6 * D_MODEL), 0.02),
        'g_norm1': 1.0 + nrm(ks[11], (DEPTH, D_MODEL), 0.02),
        'g_norm2': 1.0 + nrm(ks[12], (DEPTH, D_MODEL), 0.02),
        'w_in': nrm(ks[13], (DEPTH, D_MODEL, N_IN), D_MODEL ** -0.5),
        'g_qnorm': 1.0 + nrm(ks[14], (DEPTH, HEAD_DIM), 0.02),
        'g_knorm': 1.0 + nrm(ks[15], (DEPTH, HEAD_DIM), 0.02),
        'attn_sinks': nrm(ks[16], (DEPTH, N_HEADS), 0.5),
        'w_dw': nrm(ks[17], (DEPTH, CONV_K, CONV_W), CONV_K ** -0.5),
        'b_dw': nrm(ks[18], (DEPTH, CONV_W), 0.02),
        'g_conv_ln': 1.0 + nrm(ks[19], (DEPTH, CONV_W), 0.02),
        'b_conv_ln': nrm(ks[20], (DEPTH, CONV_W), 0.02),
        'w_pool': nrm(ks[21], (DEPTH, POOL_GROUPS, POOL_G, POOL_G), POOL_G ** -0.5),
        's_pool': 1.0 + nrm(ks[22], (DEPTH, POOL_W), 0.02),
        'g_ret_norm': 1.0 + nrm(ks[23], (DEPTH, BR_W), 0.02),
        'w_br': nrm(ks[24], (DEPTH, N_BR, BR_W, D_MODEL), BR_W ** -0.5),
        'w_out': nrm(ks[25], (DEPTH, D_MODEL, D_MODEL), D_MODEL ** -0.5),
        'w_mlp1': nrm(ks[26], (DEPTH, D_MODEL, D_FF), D_MODEL ** -0.5),
        'w_mlp2': nrm(ks[27], (DEPTH, D_FF, D_MODEL), D_FF ** -0.5),
    }


def reference(x_prompt, x_sample, c_prompt, c_sample, cache_attn_k, cache_attn_v, state_conv, state_pool, state_ret,
              w_ada, b_ada, g_norm1, g_norm2, w_in, g_qnorm, g_knorm, attn_sinks, w_dw, b_dw, g_conv_ln, b_conv_ln,
              w_pool, s_pool, g_ret_norm, w_br, w_out, w_mlp1, w_mlp2):
    yp, ys = x_prompt, x_sample
    new_p = [[], [], [], [], []]
    new_s = [[], [], [], [], []]
    for l in range(DEPTH):
        p = (w_ada[l], b_ada[l], g_norm1[l], g_norm2[l], w_in[l], g_qnorm[l], g_knorm[l], attn_sinks[l], w_dw[l],
             b_dw[l], g_conv_ln[l], b_conv_ln[l], w_pool[l], s_pool[l], g_ret_norm[l], w_br[l], w_out[l],
             w_mlp1[l], w_mlp2[l])
        yp, st_p = trunk_layer(yp, c_prompt, p, None)
        ys, st_s = trunk_layer(ys, c_sample, p, (cache_attn_k[l], cache_attn_v[l], state_conv[l], state_pool[l], state_ret[l]))
        for lst, a in zip(new_p, st_p):
            lst.append(a)
        for lst, a in zip(new_s, st_s):
            lst.append(a)
    new_attn_k_prompt = jnp.stack(new_p[0])
    new_attn_v_prompt = jnp.stack(new_p[1])
    new_conv_prompt = jnp.stack(new_p[2])
    new_pool_prompt = jnp.stack(new_p[3])
    new_ret_prompt = jnp.stack(new_p[4])
    new_attn_k_sample = jnp.stack(new_s[0])
    new_attn_v_sample = jnp.stack(new_s[1])
    new_conv_sample = jnp.stack(new_s[2])
    new_pool_sample = jnp.stack(new_s[3])
    new_ret_sample = jnp.stack(new_s[4])
    return (yp, ys, new_attn_k_prompt, new_attn_v_prompt, new_conv_prompt, new_pool_prompt, new_ret_prompt,
            new_attn_k_sample, new_attn_v_sample, new_conv_sample, new_pool_sample, new_ret_sample)
```

```python
import os
from contextlib import ExitStack
import numpy as np
import concourse.bass as bass
import concourse.mybir as mybir
from concourse.bass_utils import run_bass_kernel_spmd

F32 = mybir.dt.float32
BF16 = mybir.dt.bfloat16
AF = mybir.ActivationFunctionType
ALU = mybir.AluOpType

D = 2048
KC = 16
SEQ = 4096
DEPTH = 4
NPT = 512
NPASS = SEQ // NPT
SS = 4
ST = 8
NST = SS * ST
N_IN = 15872
EPS = 1e-6
WEL = 4096
NSLOT = 3
OQ, OK_, OV, OC, OP, ORQ, ORK, ORV, ORG, OG = 0, 1024, 1280, 1536, 3584, 4608, 5120, 5632, 6656, 7680
POOL_W = (2, 4, 8, 16)


class Chan:
    def __init__(self, sem):
        self.sem = sem
        self.cnt = 0


class Sched:
    def __init__(self, nc):
        self.nc = nc
        self.eng = {"pe": nc.tensor, "act": nc.scalar, "dve": nc.vector, "sp": nc.sync}
        self.sem = {}
        self.cnt = {}
        self.waited = {e: {} for e in self.eng}
        self.last_w = {}
        self.readers = {}
        self.arena_dma = []
        self.nsem = 0
        self.new_epoch()

    def alloc_sem(self, name):
        self.nsem += 1
        return self.nc.alloc_semaphore(name=f"{name}_{self.nsem}")

    def new_epoch(self):
        for e in ("pe", "act", "dve"):
            self.sem[e] = self.alloc_sem("p" + e)
            self.cnt[e] = 0

    def _wait(self, e, toks):
        eng = self.eng[e]
        wd = self.waited[e]
        for (sem, val) in toks:
            if e == "pe" and sem is self.sem.get("pe"):
                continue
            sid = id(sem)
            if wd.get(sid, (None, 0))[1] >= val:
                continue
            eng.wait_ge(sem, val)
            wd[sid] = (sem, val)

    def _deps(self, reads, writes):
        need = []
        for k in reads:
            w = self.last_w.get(k)
            if w:
                need.append(w)
        for k in writes:
            w = self.last_w.get(k)
            if w:
                need.append(w)
            need.extend(self.readers.get(k, {}).values())
        return need

    def _record(self, tok, reads, writes):
        for k in writes:
            self.last_w[k] = tok
            self.readers[k] = {}
        for k in reads:
            if k in writes:
                continue
            d = self.readers.setdefault(k, {})
            sid = id(tok[0])
            if d.get(sid, (None, 0))[1] < tok[1]:
                d[sid] = tok

    def op(self, e, fn, reads=(), writes=(), signal=True):
        self._wait(e, self._deps(reads, writes))
        ins = fn()
        if signal:
            self.cnt[e] += 1
            ins.then_inc(self.sem[e], 1)
            tok = (self.sem[e], self.cnt[e])
        else:
            tok = (self.sem[e], self.cnt[e] + 1)
        self._record(tok, reads, writes)
        return tok

    def dma(self, q, fn, chan, reads=(), writes=(), arena=False):
        self._wait(q, self._deps(reads, writes))
        ins = fn()
        chan.cnt += 16
        ins.then_inc(chan.sem, 16)
        tok = (chan.sem, chan.cnt)
        self._record(tok, reads, writes)
        if arena:
            self.arena_dma.append(tok)
        return tok

    def fence(self):
        toks = [(self.sem[e], self.cnt[e]) for e in ("pe", "act", "dve") if self.cnt[e] > 0]
        toks += self.arena_dma
        self.arena_dma = []
        for e in ("pe", "act", "dve", "sp"):
            eng = self.eng[e]
            wd = self.waited[e]
            for (sem, val) in toks:
                if sem is self.sem.get(e):
                    continue
                sid = id(sem)
                if wd.get(sid, (None, 0))[1] >= val:
                    continue
                eng.wait_ge(sem, val)
                wd[sid] = (sem, val)


class WStream:
    def __init__(self, sc, nc, slots):
        self.sc, self.nc, self.slots = sc, nc, slots
        self.chans = [Chan(sc.alloc_sem("ws")) for _ in slots]
        self.loads = []
        self.i = 0

    def get(self, pieces):
        s = self.i % len(self.slots)
        self.i += 1
        key = ("w", s)
        prev_readers = list(self.sc.readers.get(key, {}).values())
        ch = self.chans[s]
        ch.cnt += 16 * len(pieces)
        tok = (ch.sem, ch.cnt)
        self.sc.last_w[key] = tok
        self.sc.readers[key] = {}
        self.loads.append((s, pieces, prev_readers))
        return self.slots[s], key

    def flush(self):
        g = self.nc.gpsimd
        waited = {}
        for (s, pieces, prev) in self.loads:
            for (sem, val) in prev:
                if waited.get(id(sem), 0) >= val:
                    continue
                g.wait_ge(sem, val)
                waited[id(sem)] = val
            for (dst, src) in pieces:
                g.dma_start(out=dst, in_=src).then_inc(self.chans[s].sem, 16)


def host_tables():
    t = {}
    s = np.arange(128)[:, None].astype(np.float64)
    q = np.arange(128)[None, :].astype(np.float64)
    slopes = np.exp2(-8.0 * np.arange(1, 17) / 16.0)
    ta = np.zeros((128, 16, 2, 128), np.float64)
    for h in range(16):
        dist = 128 + q - s
        ta[:, h, 0, :] = np.where(q <= s, np.exp(-slopes[h] * dist), 0.0)
        dist = q - s
        ta[:, h, 1, :] = np.where(q >= s, np.exp(-slopes[h] * dist), 0.0)
    t["tab_attn"] = ta.reshape(128, 16 * 2 * 128).astype(np.float32)
    gam = 1.0 - np.exp2(-5.0 - np.arange(8))
    lg = np.log1p(-np.exp2(-5.0 - np.arange(8)))
    td = np.zeros((128, 8, 128), np.float64)
    for h in range(8):
        td[:, h, :] = np.where(q >= s, np.exp(lg[h] * np.maximum(q - s, 0)), 0.0) * 0.125
    t["tab_dec"] = td.reshape(128, 8 * 128).astype(np.float32)
    gp = np.zeros((128, 4, 128), np.float64)
    for j in range(4):
        for half in range(2):
            h = 2 * j + half
            gp[64 * half:64 * half + 64, j, :] = np.exp(lg[h] * (np.arange(128) + 1))[None, :]
    t["tab_gpow"] = gp.reshape(128, 4 * 128).astype(np.float32)
    kd = np.zeros((128, 2, 8, 64), np.float64)
    for h in range(8):
        kd[:, 0, h, :] = (np.exp(lg[h] * (127 - np.arange(128))) * 0.125)[:, None]
        kd[:8, 1, h, :] = (np.exp(lg[h] * (7 - np.arange(8))) * 0.125)[:, None]
    t["tab_kdec"] = kd.reshape(128, 2 * 512).astype(np.float32)
    gc = np.zeros((128, 2, 4), np.float64)
    for j in range(4):
        for half in range(2):
            h = 2 * j + half
            gc[64 * half:64 * half + 64, 0, j] = np.exp(lg[h] * 128)
            gc[64 * half:64 * half + 64, 1, j] = np.exp(lg[h] * 8)
    t["tab_gc"] = gc.reshape(128, 8).astype(np.float32)
    ic = np.zeros((128, 4, 16), np.float64)
    for g, w in enumerate(POOL_W):
        ic[:, g, :] = 1.0 / np.minimum(np.arange(16) + 1, w)[None, :]
    t["tab_icnt"] = ic.reshape(128, 64).astype(np.float32)
    mats = np.zeros((128, 8, 128), np.float32)
    mats[:, 0, :] = np.eye(128)
    mats[:, 1, :] = 1.0 / 2048
    mats[:, 2, :] = 1.0 / 1024
    mats[:, 3, :] = 1.0 / 128
    mats[:64, 4, :64] = 1.0 / 64
    mats[64:, 4, 64:] = 1.0 / 64
    mats[:, 5, :64] = 1.0
    mats[:, 6, 64:] = 1.0
    t["tab_mats"] = mats.reshape(128, 8 * 128)
    return t


def build_program(nlayers=DEPTH, npass=NPASS, do_sample=True, wdepth=DEPTH):
    nc = bass.Bass("TRN2", target_bir_lowering=False)
    es = ExitStack()

    def din(name, shape):
        return nc.dram_tensor(name, list(shape), F32, kind="ExternalInput").ap()

    def dout(name, shape):
        return nc.dram_tensor(name, list(shape), F32, kind="ExternalOutput").ap()

    xT = din("xT", [D, SEQ])
    xsT = din("xsT", [D, NST])
    cT = din("cT", [128, KC * 5])
    ck = din("ck", [DEPTH, SS, 128, 256])
    cv = din("cv", [DEPTH, SS, 128, 256])
    stc = din("stc", [DEPTH, SS, 30, 1024])
    stp = din("stp", [DEPTH, SS, 15, 1024])
    strt = din("strt", [DEPTH, SS, 8, 64, 128])
    w_ada = din("w_ada", [wdepth, D, 6 * D])
    b_ada = din("b_ada", [DEPTH, 6 * D])
    g_norm1 = din("g_norm1", [DEPTH, D])
    g_norm2 = din("g_norm2", [DEPTH, D])
    w_in = din("w_in", [wdepth, D, N_IN])
    g_qn = din("g_qnorm", [DEPTH, 64])
    g_kn = din("g_knorm", [DEPTH, 64])
    sinks = din("attn_sinks", [DEPTH, 16])
    w_dw = din("w_dw", [DEPTH, 31, 1024])
    b_dw = din("b_dw", [DEPTH, 1024])
    g_cln = din("g_conv_ln", [DEPTH, 1024])
    b_cln = din("b_conv_ln", [DEPTH, 1024])
    w_pool = din("w_pool", [DEPTH, 4, 256, 256])
    s_pool = din("s_pool", [DEPTH, 1024])
    g_ret = din("g_ret_norm", [DEPTH, 1024])
    w_br = din("w_br", [wdepth, 4, 1024, D])
    w_out = din("w_out", [wdepth, D, D])
    w_mlp1 = din("w_mlp1", [wdepth, D, 4 * D])
    w_mlp2 = din("w_mlp2", [wdepth, 4 * D, D])
    tabs = host_tables()
    tin = {k: din(k, v.shape) for k, v in tabs.items()}

    o_yT = dout("o_yT", [D, SEQ])
    o_ysT = dout("o_ysT", [D, NST])
    o_kp = dout("o_kp", [DEPTH, 4, 64, 128])
    o_vp = dout("o_vp", [DEPTH, 128, 256])
    o_cp = dout("o_cp", [DEPTH, 1024, 30])
    o_pp = dout("o_pp", [DEPTH, 1024, 15])
    o_rp = dout("o_rp", [DEPTH, 8, 64, 128])
    o_ks = dout("o_ks", [DEPTH, SS, 128, 256])
    o_vs = dout("o_vs", [DEPTH, SS, 128, 256])
    o_cs = dout("o_cs", [DEPTH, SS, 1024, 30])
    o_psm = dout("o_psm", [DEPTH, SS, 1024, 15])
    o_rs = dout("o_rs", [DEPTH, SS, 8, 64, 128])

    sc = Sched(nc)
    PE, ACT, DVE, SP = "pe", "act", "dve", "sp"

    uid = [0]

    def sb(name, shape, dt=F32, stack=es):
        uid[0] += 1
        return stack.enter_context(nc.sbuf_tensor(f"{name}_{uid[0]}", list(shape), dt))

    xr = sb("xr", [128, KC, NPT])
    hT = sb("hT", [128, KC, NPT], BF16)
    merged = sb("merged", [128, KC, NPT])
    brT = sb("brT", [128, 8, NPT], BF16)
    wslots = [sb(f"wslot{i}", [128, WEL], BF16) for i in range(NSLOT)]
    adaT = sb("adaT", [128, DEPTH * 96 * 5])
    A1 = sb("A1", [128, DEPTH * KC * 5])
    A2 = sb("A2", [128, DEPTH * KC * 5])
    NG = 18
    constT = sb("constT", [128, NG, 128])
    gqk = sb("gqk", [128, 2 * DEPTH])
    esink = sb("esink", [128, DEPTH * 8])
    t_attn = sb("t_attn", [128, 16 * 2 * 128], BF16)
    t_gc = sb("t_gc", [128, 8])
    t_icnt = sb("t_icnt", [128, 64])
    ident = sb("ident", [128, 128])
    mats = sb("matsb", [128, 7, 128], BF16)
    st_conv = sb("st_conv", [128, DEPTH, 8, 30])
    st_pool = sb("st_pool", [128, DEPTH, 8, 15])
    st_k = sb("st_k", [128, DEPTH, 4, 128], BF16)
    st_v = sb("st_v", [128, DEPTH, 4, 64], BF16)
    st_S = sb("st_S", [128, DEPTH, 4, 128])
    Spad = sb("Spad", [128, 8, 128], BF16)
    epsb = sb("epsb", [128, 1])

    banks = [es.enter_context(nc.psum_tensor(f"ps{i}", [128, 512], F32)) for i in range(8)]
    rr = [0]

    def gbank():
        b = rr[0] % 4
        rr[0] += 1
        return b

    def PS(b):
        return ("ps", b)

    misc = [Chan(sc.alloc_sem("mi")) for _ in range(8)]
    mi = [0]

    def mchan():
        c = misc[mi[0] % len(misc)]
        mi[0] += 1
        return c

    outch = Chan(sc.alloc_sem("out"))
    W = WStream(sc, nc, wslots)

    def act(out, in_, func, reads, writes, bias=None, scale=None):
        kw = {}
        if bias is not None:
            kw["bias"] = bias
        if scale is not None:
            kw["scale"] = scale
        return sc.op(ACT, lambda: nc.scalar.activation(out, in_, func, **kw), reads, writes)

    def tt(out, a, b, op, reads, writes):
        return sc.op(DVE, lambda: nc.vector.tensor_tensor(out, a, b, op), reads, writes)

    def ts(out, a, s1, s2, op0, op1, reads, writes):
        if op1 is None:
            return sc.op(DVE, lambda: nc.vector.tensor_scalar(out, a, s1, None, op0), reads, writes)
        return sc.op(DVE, lambda: nc.vector.tensor_scalar(out, a, s1, s2, op0, op1), reads, writes)

    def stt(out, a, s, b, op0, op1, reads, writes):
        return sc.op(DVE, lambda: nc.vector.scalar_tensor_tensor(out, a, s, b, op0, op1), reads, writes)

    def vcopy(out, in_, reads, writes):
        return sc.op(DVE, lambda: nc.vector.tensor_copy(out, in_), reads, writes)

    def recip(out, in_, reads, writes):
        return sc.op(DVE, lambda: nc.vector.reciprocal(out, in_), reads, writes)

    def mm(out, lhsT, rhs, start, stop, reads, writes, signal=None):
        signal = True
        return sc.op(PE, lambda: nc.tensor.matmul(out, lhsT, rhs, start=start, stop=stop), reads, writes, signal=signal)

    def transp(out, in_, reads, writes):
        return sc.op(PE, lambda: nc.tensor.transpose(out, in_, ident[:in_.shape[0], :in_.shape[0]]), reads, writes)

    def ld(out, in_, writes, q=SP, arena=False, **kw):
        return sc.dma(q, lambda: sc.eng[q].dma_start(out=out, in_=in_, **kw), mchan(), (), writes, arena=arena)

    def st(out, in_, reads, arena=False, **kw):
        return sc.dma(SP, lambda: nc.sync.dma_start(out=out, in_=in_, **kw), outch, reads, (), arena=arena)

    sc.op(DVE, lambda: nc.vector.memset(epsb[:], EPS), (), ["epsb"])
    for t_, nm in ((t_gc, "tab_gc"), (t_icnt, "tab_icnt")):
        ld(t_[:], tin[nm][:, :], [nm])
    ld(ident[:], tin["tab_mats"][:, 0:128], ["ident"])
    with ExitStack() as ph:
        stg = sb("stg", [128, NG, 128], F32, ph)
        tmpa = sb("tmpa", [128, 16 * 2 * 128], F32, ph)
        tmpm = sb("tmpm", [128, 7 * 128], F32, ph)
        cTs = sb("cTs", [128, KC * 5], F32, ph)
        scT = sb("scT", [128, KC * 5], BF16, ph)
        sk_raw = sb("sk_raw", [128, DEPTH * 8], F32, ph)
        gtmp = sb("gtmp", [128, 2 * DEPTH], F32, ph)
        sc.op(DVE, lambda: nc.vector.memset(stg[:], 0.0), (), ["stg"])
        ld(tmpa[:], tin["tab_attn"][:, :], ["tmpa"])
        vcopy(t_attn[:], tmpa[:], ["tmpa"], ["t_attn"])
        ld(tmpm[:], tin["tab_mats"][:, 0:7 * 128], ["tmpm"])
        vcopy(mats[:].rearrange("p a b -> p (a b)"), tmpm[:], ["tmpm"], ["mats"])
        rows = {}
        ba = b_ada.rearrange("l (b p) -> (l b) p", p=128)
        for g in range(3):
            ld(stg[:, g, :], ba[g * 128:(g + 1) * 128, :], ["stg"])
        ld(stg[0:64, 3, :], g_norm1.rearrange("l (b p) -> (l b) p", p=128), ["stg"])
        ld(stg[0:64, 4, :], g_norm2.rearrange("l (b p) -> (l b) p", p=128), ["stg"])
        for i, prm in enumerate((b_dw, g_cln, b_cln, s_pool, g_ret)):
            ld(stg[0:32, 5 + i, :], prm.rearrange("l (b p) -> (l b) p", p=128), ["stg"])
        wd = w_dw.rearrange("l j (b p) -> (l j b) p", p=128)
        for g in range(8):
            n = min(128, 992 - g * 128)
            ld(stg[0:n, 10 + g, :], wd[g * 128:g * 128 + n, :], ["stg"])
        for g in range(NG):
            b = gbank()
            transp(banks[b][:, 0:128], stg[:, g, :], ["stg"], [PS(b)])
            act(constT[:, g, :], banks[b][:, 0:128], AF.Copy, [PS(b)], ["constT"])

        def cB_ada(l, blk):
            r = l * 96 + blk
            return constT[:, r // 128, r % 128:r % 128 + 1]

        for half in range(2):
            ld(gtmp[64 * half:64 * half + 64, 0:DEPTH], g_qn.rearrange("l d -> d l"), ["gtmp"],
               allow_slow_non_contiguous=True)
            ld(gtmp[64 * half:64 * half + 64, DEPTH:2 * DEPTH], g_kn.rearrange("l d -> d l"), ["gtmp"],
               allow_slow_non_contiguous=True)
        ts(gqk[:, 0:DEPTH], gtmp[:, 0:DEPTH], 0.125, None, ALU.mult, None, ["gtmp"], ["gqk"])
        vcopy(gqk[:, DEPTH:2 * DEPTH], gtmp[:, DEPTH:2 * DEPTH], ["gtmp"], ["gqk"])
        sk2 = sinks.rearrange("l (j two) -> two l j", two=2)
        for half in range(2):
            ld(sk_raw[64 * half:64 * half + 64, :].rearrange("p (l j) -> p l j", j=8),
               sk2[half:half + 1, :, :].broadcast_to([64, DEPTH, 8]), ["sk_raw"], allow_slow_non_contiguous=True)
        act(esink[:], sk_raw[:], AF.Exp, ["sk_raw"], ["esink"])
        ld(cTs[:], cT[:, :], ["cTs"])
        act(scT[:], cTs[:], AF.Silu, ["cTs"], ["scT"])
        scT3 = scT[:].rearrange("p (k c) -> p k c", c=5)
        for l in range(nlayers):
            wv = w_ada[l].rearrange("(kc p) n -> p kc n", p=128)
            for cb in range(6 * D // 256):
                s_ = W.i % NSLOT
                s3 = wslots[s_][:, 0:KC * 256].rearrange("p (k c) -> p k c", c=256)
                _, key = W.get([(s3, wv[:, :, cb * 256:(cb + 1) * 256])])
                for sub in range(2):
                    blk = cb * 2 + sub
                    b = gbank()
                    for kc in range(KC):
                        mm(banks[b][:, 0:5], s3[:, kc, sub * 128:(sub + 1) * 128], scT3[:, kc, :],
                           kc == 0, kc == KC - 1, [key, "scT"], [PS(b)])
                    o = (l * 96 + blk) * 5
                    act(adaT[:, o:o + 5], banks[b][:, 0:5], AF.Identity, [PS(b), "constT"], ["adaT"],
                        bias=cB_ada(l, blk))
        ada4 = adaT[:].rearrange("p (l s k c) -> p l s k c", l=DEPTH, s=6, k=KC)
        for l in range(DEPTH):
            for (Adst, sidx, grp) in ((A1, 1, 3), (A2, 4, 4)):
                for kc in range(KC):
                    r = l * KC + kc
                    o = (l * KC + kc) * 5
                    stt(Adst[:, o:o + 5], ada4[:, l, sidx, kc, :], 1.0,
                        constT[:, grp, r:r + 1].broadcast_to([128, 5]), ALU.add, ALU.mult,
                        ["adaT", "constT"], ["A12"])
        sc.fence()

    ada4 = adaT[:].rearrange("p (l s k c) -> p l s k c", l=DEPTH, s=6, k=KC)
    A1v = A1[:].rearrange("p (l k c) -> p l k c", l=DEPTH, k=KC)
    A2v = A2[:].rearrange("p (l k c) -> p l k c", l=DEPTH, k=KC)

    def cvec(grp, l, blk, nb=8):
        r = l * nb + blk
        return constT[:, grp, r:r + 1]

    def cwdw(l, j, blk):
        r = (l * 31 + j) * 8 + blk
        return constT[:, 10 + r // 128, r % 128:r % 128 + 1]

    KSTOP = os.environ.get("KSTOP", "")

    class StopBuild(Exception):
        pass

    def chk(stage):
        if KSTOP == stage:
            raise StopBuild()

    def colgroups(sample):
        return [(8 * s, 8, 1 + s) for s in range(SS)] if sample else [(0, NPT, 0)]

    def rmsnorm_mod(l, N, sample, Av, sh_idx, ph):
        sq = [sb(f"nsq{i}", [128, N], BF16, ph) for i in range(2)]
        tmp = [sb(f"ntmp{i}", [128, N], F32, ph) for i in range(2)]
        rstd = sb("nrstd", [128, N], F32, ph)
        b = gbank()
        for kc in range(KC):
            s_ = sq[kc % 2]
            act(s_[:, :N], xr[:, kc, :N], AF.Square, ["xr"], [("nsq", kc % 2)])
            mm(banks[b][:, :N], mats[:, 1, :], s_[:, :N], kc == 0, kc == KC - 1, [("nsq", kc % 2)], [PS(b)])
        act(rstd[:, :N], banks[b][:, :N], AF.Sqrt, [PS(b)], ["nrstd"], bias=epsb[:, 0:1])
        recip(rstd[:, :N], rstd[:, :N], ["nrstd"], ["nrstd"])
        for kc in range(KC):
            t_ = tmp[kc % 2]
            tt(t_[:, :N], xr[:, kc, :N], rstd[:, :N], ALU.mult, ["xr", "nrstd"], [("ntmp", kc % 2)])
            for (c0, n, col) in colgroups(sample):
                act(hT[:, kc, c0:c0 + n], t_[:, c0:c0 + n], AF.Identity, [("ntmp", kc % 2)], ["hT"],
                    bias=ada4[:, l, sh_idx, kc, col:col + 1], scale=Av[:, l, kc, col:col + 1])

    def wtile_cols(wv, c0, ncols=256):
        s = W.i % NSLOT
        dst = wslots[s][:, 0:KC * ncols].rearrange("p (k c) -> p k c", c=ncols)
        slot, key = W.get([(dst, wv[:, :, c0:c0 + ncols])])
        return dst, key

    def proj_fm(w3, sub, key, N, bank, nk=KC, rhs=None):
        for kc in range(nk):
            r = hT[:, kc, :N] if rhs is None else rhs(kc)
            mm(banks[bank][:, :N], w3[:, kc, sub * 128:(sub + 1) * 128], r, kc == 0, kc == nk - 1,
               [key, "hT" if rhs is None else "rhsx"], [PS(bank)])

    def headnorm(l, N, bank, gcol, dst, ph_bufs, extra_f32=None):
        raw, sq, rs = ph_bufs
        act(raw[:, :N], banks[bank][:, :N], AF.Copy, [PS(bank)], ["hn_raw"])
        act(sq[:, :N], banks[bank][:, :N], AF.Square, [PS(bank)], ["hn_sq"])
        b2 = gbank()
        mm(banks[b2][:, :N], mats[:, 4, :], sq[:, :N], True, True, ["hn_sq"], [PS(b2)])
        act(rs[:, :N], banks[b2][:, :N], AF.Sqrt, [PS(b2)], ["hn_rs"], bias=epsb[:, 0:1])
        recip(rs[:, :N], rs[:, :N], ["hn_rs"], ["hn_rs"])
        stt(dst, raw[:, :N], gqk[:, gcol:gcol + 1], rs[:, :N], ALU.mult, ALU.mult, ["hn_raw", "hn_rs"], ["hn_dst"])
        if extra_f32 is not None:
            (dstf, c0, n) = extra_f32
            stt(dstf, raw[:, c0:c0 + n], gqk[:, gcol:gcol + 1], rs[:, c0:c0 + n], ALU.mult, ALU.mult,
                ["hn_raw", "hn_rs"], ["hn_dstf"])

    def merge_branch(l, r, N):
        wb = w_br[l, r].rearrange("(kc p) n -> p kc n", p=128)
        wg = w_in[l].rearrange("(kc p) n -> p kc n", p=128)
        with ExitStack() as ph:
            sig = [sb(f"msig{i}", [128, N], F32, ph) for i in range(2)]
            tmp = [sb(f"mtmp{i}", [128, N], F32, ph) for i in range(2)]
            for dq in range(4):
                s = W.i % NSLOT
                dstb = wslots[s][:, 0:8 * 512].rearrange("p (k c) -> p k c", c=512)
                _, keyb = W.get([(dstb, wb[:, :, dq * 512:(dq + 1) * 512])])
                gts = []
                for gq in range(2):
                    gts.append(wtile_cols(wg, OG + r * D + dq * 512 + gq * 256))
                for sub in range(4):
                    db = dq * 4 + sub
                    i2 = db % 2
                    bg = gbank()
                    g3, keyg = gts[sub // 2]
                    proj_fm(g3, sub % 2, keyg, N, bg)
                    act(sig[i2][:, :N], banks[bg][:, :N], AF.Sigmoid, [PS(bg)], [("msig", i2)])
                    bp = gbank()
                    for kc in range(8):
                        mm(banks[bp][:, :N], dstb[:, kc, sub * 128:(sub + 1) * 128], brT[:, kc, :N],
                           kc == 0, kc == 7, [keyb, "brT"], [PS(bp)])
                    if r == 0:
                        tt(merged[:, db, :N], banks[bp][:, :N], sig[i2][:, :N], ALU.mult,
                           [PS(bp), ("msig", i2)], [("merged", db)])
                    else:
                        tt(tmp[i2][:, :N], banks[bp][:, :N], sig[i2][:, :N], ALU.mult,
                           [PS(bp), ("msig", i2)], [("mtmp", i2)])
                        tt(merged[:, db, :N], merged[:, db, :N], tmp[i2][:, :N], ALU.add,
                           [("mtmp", i2), ("merged", db)], [("merged", db)])
        sc.fence()

    def attention(l, N, sample, first, last, pi):
        L = ST if sample else 128
        wv = w_in[l].rearrange("(kc p) n -> p kc n", p=128)
        with ExitStack() as ph:
            qT = sb("a_qT", [128, 8, N], BF16, ph)
            kT = sb("a_kT", [128, 4, 128 + N + 32], BF16, ph)
            kF = sb("a_kF", [128, 4, 128], F32, ph)
            Vp = sb("a_Vp", [128, 5, 4, 2, 128], BF16, ph)
            vF = sb("a_vF", [128, 4 if sample else 1, 256], F32, ph)
            raw = sb("a_raw", [128, N], F32, ph)
            sq = sb("a_sq", [128, N], BF16, ph)
            rs = sb("a_rs", [128, N], F32, ph)
            E = [sb(f"a_E{i}", [128, 256], F32, ph) for i in range(2)]
            Pb = [sb(f"a_P{i}", [128, 256], BF16, ph) for i in range(2)]
            den = sb("a_den", [128, 128], F32, ph)
            if sample:
                kc_sb = sb("a_kc", [128, 2, 4, 2, 64], F32, ph)
                vc_sb = sb("a_vc", [128, 2, 256], F32, ph)
                kTs = sb("a_kTs", [128, SS, 4, 128], BF16, ph)
                Vps = sb("a_Vps", [128, SS, 4, 2, 128], BF16, ph)
            sc.op(DVE, lambda: nc.vector.memset(Vp[:].rearrange("p a b c d -> p (a b c d)"), 0.0), (), ["Vp"])
            if sample:
                for i2 in range(2):
                    sc.op(DVE, lambda: nc.vector.memset(Pb[i2][:], 0.0), (), [("aP", i2)])
                sc.op(DVE, lambda: nc.vector.memset(kT[:].rearrange("p a b -> p (a b)"), 0.0), (), ["kT", "hn_dst"])
            if sample:
                sc.op(DVE, lambda: nc.vector.memset(Vps[:].rearrange("p a b c d -> p (a b c d)"), 0.0), (), ["Vps"])
                for s in range(SS):
                    i2 = s % 2
                    for dup in range(2):
                        ld(kc_sb[:, i2, :, dup, :], ck[l, s].rearrange("t (k d) -> t k d", d=64), [("kc_sb", i2)], arena=True)
                    ld(vc_sb[:, i2, :], cv[l, s], [("vc_sb", i2)], arena=True)
                    if not os.environ.get("KNO_D2D"):
                        st(o_ks[l, s, 0:120, :], ck[l, s, 8:128, :], [])
                        st(o_vs[l, s, 0:120, :], cv[l, s, 8:128, :], [])
                    for kv in range(4):
                        b = gbank()
                        transp(banks[b][:, 0:128], kc_sb[:, i2, kv, :, :].rearrange("p a d -> p (a d)"),
                               [("kc_sb", i2)], [PS(b)])
                        act(kTs[:, s, kv, :], banks[b][:, 0:128], AF.Copy, [PS(b)], ["kTs"])
                        vsrc = vc_sb[:, i2, kv * 64:(kv + 1) * 64]
                        vcopy(Vps[:, s, kv, 0, 0:64], vsrc, [("vc_sb", i2)], ["Vps"])
                        vcopy(Vps[:, s, kv, 1, 64:128], vsrc, [("vc_sb", i2)], ["Vps"])
            else:
                if not first:
                    vcopy(kT[:, :, 0:128], st_k[:, l, :, :], [("st_k", l)], ["kT"])
                    for kv in range(4):
                        vcopy(Vp[:, 0, kv, 0, 0:64], st_v[:, l, kv, :], [("st_v", l)], ["Vp"])
                        vcopy(Vp[:, 0, kv, 1, 64:128], st_v[:, l, kv, :], [("st_v", l)], ["Vp"])
            chk("a1")
            for cb in range(4):
                w3, key = wtile_cols(wv, OQ + cb * 256)
                for sub in range(2):
                    b = gbank()
                    proj_fm(w3, sub, key, N, b)
                    headnorm(l, N, b, l, qT[:, cb * 2 + sub, :N], (raw, sq, rs))
            for half in range(2):
                s_ = W.i % NSLOT
                kd = wslots[s_][:, 0:KC * 256].rearrange("p (k c) -> p k c", c=256)
                kd5 = wslots[s_][:, 0:KC * 256].rearrange("p (k v t d) -> p k v t d", v=2, t=2, d=64)
                pieces = [(kd5[:, :, v, dup, :],
                           wv[:, :, OK_ + half * 128 + v * 64:OK_ + half * 128 + v * 64 + 64])
                          for dup in range(2) for v in range(2)]
                _, key = W.get(pieces)
                for sub in range(2):
                    kv = half * 2 + sub
                    b = gbank()
                    proj_fm(kd, sub, key, N, b)
                    extra = None
                    if sample:
                        extra = (kF[:, kv, 0:N], 0, N)
                    elif last:
                        extra = (kF[:, kv, :], N - 128, 128)
                    headnorm(l, N, b, DEPTH + l, kT[:, kv, 128:128 + N], (raw, sq, rs), extra)
            chk("a2")
            w3, key = wtile_cols(wv, OV)
            for blk in range(4):
                c0 = blk * L
                b = gbank()
                LM = max(L, 32)
                for kc in range(KC):
                    mm(banks[b][:LM, 0:256], hT[:, kc, c0:c0 + LM], w3[:, kc, :], kc == 0, kc == KC - 1,
                       [key, "hT"], [PS(b)])
                for kv in range(4):
                    if os.environ.get("KV_NOACT"):
                        break
                    if sample:
                        vcopy(Vp[:LM, 1 + blk, kv, 0, 0:64], banks[b][:LM, kv * 64:(kv + 1) * 64], [PS(b)], ["Vp"])
                        vcopy(Vp[:LM, 1 + blk, kv, 1, 64:128], banks[b][:LM, kv * 64:(kv + 1) * 64], [PS(b)], ["Vp"])
                        continue
                    act(Vp[:LM, 1 + blk, kv, 0, 0:64], banks[b][:LM, kv * 64:(kv + 1) * 64], AF.Copy, [PS(b)], ["Vp"])
                    act(Vp[:LM, 1 + blk, kv, 1, 64:128], banks[b][:LM, kv * 64:(kv + 1) * 64], AF.Copy, [PS(b)], ["Vp"])
                if (sample and not os.environ.get("KV_NOVC")) or (last and blk == 3):
                    vcopy(vF[:LM, blk if sample else 0, :], banks[b][:LM, 0:256], [PS(b)], ["vF"])
            chk("a3")
            if sample and not os.environ.get("KNO_KOUT"):
                for s in range(SS):
                    st(o_vs[l, s, 120:128, :], vF[:ST, s, :], ["vF"], arena=True)
                    for kv in range(4):
                        st(o_ks[l, s, 120:128, kv * 64:(kv + 1) * 64].rearrange("t d -> d t"),
                           kF[0:64, kv, s * ST:(s + 1) * ST], ["hn_dstf"], arena=True, allow_slow_non_contiguous=True)
            elif last:
                st(o_vp[l], vF[:, 0, :], ["vF"], arena=True)
                st(o_kp[l].rearrange("k d t -> d k t"), kF[0:64, :, :], ["hn_dstf"], arena=True)
            if (not sample) and (not last):
                vcopy(st_k[:, l, :, :], kT[:, :, NPT:NPT + 128], ["hn_dst"], [("st_k", l)])
                for kv in range(4):
                    vcopy(st_v[:, l, kv, :], Vp[:, 4, kv, 0, 0:64], ["Vp"], [("st_v", l)])
            chk("a4")
            for blk in range(4):
                c0 = blk * L
                has_prev = sample or (not (first and blk == 0))
                for j in range(8):
                    kv = j // 2
                    for half in range(2):
                        hd = 2 * j + half
                        p0 = 64 * half
                        bs = 4 + half
                        i2 = half
                        qv = qT[p0:p0 + 64, j, c0:c0 + L]
                        if has_prev:
                            kprev = kTs[p0:p0 + 64, blk, kv, :] if sample else kT[p0:p0 + 64, kv, c0:c0 + 128]
                            mm(banks[bs][:, 0:L], kprev, qv, True, True, ["hn_dst", "kTs", "kT"], [PS(bs)], signal=False)
                        LM = max(L, 32)
                        kown = kT[p0:p0 + 64, kv, 128 + c0:128 + c0 + LM]
                        mm(banks[bs][:LM, 128:128 + L], kown, qv, True, True, ["hn_dst", "kT"], [PS(bs)])
                        tab = t_attn[:].rearrange("p (h a q) -> p h a q", h=16, a=2)
                        if has_prev:
                            act(E[i2][:, 0:L], banks[bs][:, 0:L], AF.Exp, [PS(bs)], [("aE", i2)])
                            tt(Pb[i2][:, 0:L], E[i2][:, 0:L], tab[:, hd, 0, 0:L], ALU.mult, [("aE", i2)], [("aP", i2)])
                        act(E[i2][:LM, 128:128 + L], banks[bs][:LM, 128:128 + L], AF.Exp, [PS(bs)], [("aE", i2)])
                        tt(Pb[i2][:LM, 128:128 + L], E[i2][:LM, 128:128 + L], tab[:LM, hd, 1, 0:L], ALU.mult,
                           [("aE", i2)], [("aP", i2)])
                    bo = 6
                    seqs = []
                    for half in range(2):
                        if has_prev:
                            seqs.append((half, 0))
                        seqs.append((half, 1))
                    for which in range(2):
                        for n_, (half, po) in enumerate(seqs):
                            if po == 0:
                                rhs = Pb[half][:, 0:L]
                                if which == 0:
                                    lhs = Vps[:, blk, kv, half, :] if sample else Vp[:, blk, kv, half, :]
                                else:
                                    lhs = mats[:, 5 + half, :]
                            else:
                                rhs = Pb[half][:, 128:128 + L]
                                lhs = Vp[:, 1 + blk, kv, half, :] if which == 0 else mats[:, 5 + half, :]
                            mm(banks[bo][:, which * 128:which * 128 + L], lhs, rhs, n_ == 0, n_ == len(seqs) - 1,
                               [("aP", half), "Vp", "Vps"], [PS(bo)])
                    ts(den[:, 0:L], banks[bo][:, 128:128 + L], esink[:, l * 8 + j:l * 8 + j + 1], None, ALU.add, None,
                       [PS(bo)], ["aden"])
                    recip(den[:, 0:L], den[:, 0:L], ["aden"], ["aden"])
                    tt(brT[:, j, c0:c0 + L], banks[bo][:, 0:L], den[:, 0:L], ALU.mult, [PS(bo), "aden"], ["brT"])
        sc.fence()

    def conv_module(l, N, sample, first, last):
        wv = w_in[l].rearrange("(kc p) n -> p kc n", p=128)
        L = ST if sample else N
        nseg = SS if sample else 1
        EW = 30 + L
        with ExitStack() as ph:
            ext = sb("c_ext", [128, 4, nseg * EW], F32, ph)
            y = sb("c_y", [128, 8, N], F32, ph)
            sig = [sb(f"c_sig{i}", [128, N], F32, ph) for i in range(2)]
            ybf = [sb(f"c_ybf{i}", [128, N], BF16, ph) for i in range(2)]
            ysq = [sb(f"c_ysq{i}", [128, N], BF16, ph) for i in range(2)]
            mean = sb("c_mean", [128, N], F32, ph)
            rstd = sb("c_rstd", [128, N], F32, ph)
            tmp = [sb(f"c_tmp{i}", [128, N], F32, ph) for i in range(2)]
            e4 = ext[:].rearrange("p b (s e) -> p b s e", e=EW)
            stc_sb = sb("c_stc", [128, SS, 1024], F32, ph) if sample else None
            for hf in range(2):
                if sample:
                    for s in range(SS):
                        if hf == 0:
                            ld(stc_sb[0:30, s, :], stc[l, s], [("stc_sb", s)], arena=True)
                        for bb in range(4):
                            b = gbank()
                            blk = hf * 4 + bb
                            transp(banks[b][:, 0:30], stc_sb[0:30, s, blk * 128:(blk + 1) * 128], [("stc_sb", s)], [PS(b)])
                            act(e4[:, bb, s, 0:30], banks[b][:, 0:30], AF.Copy, [PS(b)], ["cext"])
                elif first:
                    sc.op(DVE, lambda: nc.vector.memset(e4[:, :, 0, 0:30], 0.0), (), ["cext"])
                else:
                    vcopy(e4[:, :, 0, 0:30], st_conv[:, l, hf * 4:hf * 4 + 4, :], [("st_conv", l)], ["cext"])
                tiles = {}
                for bb in range(4):
                    blk = hf * 4 + bb
                    if bb % 2 == 0:
                        q2 = bb // 2
                        tiles[(0, q2)] = wtile_cols(wv, OC + 1024 + hf * 512 + q2 * 256)
                        tiles[(1, q2)] = wtile_cols(wv, OC + hf * 512 + q2 * 256)
                    i2 = bb % 2
                    w3, key = tiles[(0, bb // 2)]
                    b = gbank()
                    proj_fm(w3, bb % 2, key, N, b)
                    act(sig[i2][:, :N], banks[b][:, :N], AF.Sigmoid, [PS(b)], [("csig", i2)])
                    w3, key = tiles[(1, bb // 2)]
                    b = gbank()
                    proj_fm(w3, bb % 2, key, N, b)
                    tt(e4[:, bb, :, 30:30 + L], banks[b][:, :N].rearrange("p (s t) -> p s t", t=L),
                       sig[i2][:, :N].rearrange("p (s t) -> p s t", t=L), ALU.mult, [PS(b), ("csig", i2)], ["cext"])
                    yv = y[:, blk, :N].rearrange("p (s t) -> p s t", t=L)
                    ts(yv, e4[:, bb, :, 0:L], cwdw(l, 0, blk), cvec(5, l, blk), ALU.mult, ALU.add,
                       ["cext"], [("cy", blk)])
                    for jt in range(1, 31):
                        stt(yv, e4[:, bb, :, jt:jt + L], cwdw(l, jt, blk), yv, ALU.mult, ALU.add,
                            ["cext", ("cy", blk)], [("cy", blk)])
                if sample:
                    for s in range(SS):
                        st(o_cs[l, s].rearrange("(b p) t -> p b t", p=128)[:, hf * 4:hf * 4 + 4, :],
                           e4[:, :, s, L:L + 30], ["cext"], arena=True)
                elif last:
                    st(o_cp[l].rearrange("(b p) t -> p b t", p=128)[:, hf * 4:hf * 4 + 4, :],
                       e4[:, :, 0, L:L + 30], ["cext"], arena=True)
                else:
                    vcopy(st_conv[:, l, hf * 4:hf * 4 + 4, :], e4[:, :, 0, L:L + 30], ["cext"], [("st_conv", l)])
            bm, bq = gbank(), gbank()
            for blk in range(8):
                i2 = blk % 2
                act(ybf[i2][:, :N], y[:, blk, :N], AF.Copy, [("cy", blk)], [("cybf", i2)])
                act(ysq[i2][:, :N], y[:, blk, :N], AF.Square, [("cy", blk)], [("cysq", i2)])
                mm(banks[bm][:, :N], mats[:, 2, :], ybf[i2][:, :N], blk == 0, blk == 7, [("cybf", i2)], [PS(bm)])
                mm(banks[bq][:, :N], mats[:, 2, :], ysq[i2][:, :N], blk == 0, blk == 7, [("cysq", i2)], [PS(bq)])
            act(mean[:, :N], banks[bm][:, :N], AF.Copy, [PS(bm)], ["cmean"])
            tt(rstd[:, :N], mean[:, :N], mean[:, :N], ALU.mult, ["cmean"], ["crstd"])
            tt(rstd[:, :N], banks[bq][:, :N], rstd[:, :N], ALU.subtract, [PS(bq), "crstd"], ["crstd"])
            ts(rstd[:, :N], rstd[:, :N], 0.0, None, ALU.max, None, ["crstd"], ["crstd"])
            act(rstd[:, :N], rstd[:, :N], AF.Sqrt, ["crstd"], ["crstd"], bias=epsb[:, 0:1])
            recip(rstd[:, :N], rstd[:, :N], ["crstd"], ["crstd"])
            for blk in range(8):
                i2 = blk % 2
                tt(tmp[i2][:, :N], y[:, blk, :N], mean[:, :N], ALU.subtract, [("cy", blk), "cmean"], [("ctmp", i2)])
                tt(tmp[i2][:, :N], tmp[i2][:, :N], rstd[:, :N], ALU.mult, [("ctmp", i2), "crstd"], [("ctmp", i2)])
                act(brT[:, blk, :N], tmp[i2][:, :N], AF.Silu, [("ctmp", i2)], ["brT"],
                    bias=cvec(7, l, blk), scale=cvec(6, l, blk))
        sc.fence()

    def pool_mixer(l, N, sample, first, last):
        wv = w_in[l].rearrange("(kc p) n -> p kc n", p=128)
        L = ST if sample else N
        nseg = SS if sample else 1
        EW = 15 + L
        with ExitStack() as ph:
            ext = sb("p_ext", [128, 8, nseg * EW], F32, ph)
            ta = sb("p_ta", [128, nseg * EW], F32, ph)
            tb = sb("p_tb", [128, nseg * EW], F32, ph)
            zp = sb("p_zp", [128, 8, N], BF16, ph)
            tfix = sb("p_tfix", [128, 16], F32, ph)
            e4 = ext[:].rearrange("p b (s e) -> p b s e", e=EW)
            ta3 = ta[:].rearrange("p (s e) -> p s e", e=EW)
            tb3 = tb[:].rearrange("p (s e) -> p s e", e=EW)
            if sample:
                stp_sb = sb("p_stp", [128, SS, 1024], F32, ph)
                for s in range(SS):
                    ld(stp_sb[0:15, s, :], stp[l, s], [("stp_sb", s)], arena=True)
                    for blk in range(8):
                        b = gbank()
                        transp(banks[b][:, 0:15], stp_sb[0:15, s, blk * 128:(blk + 1) * 128], [("stp_sb", s)], [PS(b)])
                        act(e4[:, blk, s, 0:15], banks[b][:, 0:15], AF.Copy, [PS(b)], ["pext"])
            elif first:
                sc.op(DVE, lambda: nc.vector.memset(e4[:, :, 0, 0:15], 0.0), (), ["pext"])
            else:
                vcopy(e4[:, :, 0, 0:15], st_pool[:, l, :, :], [("st_pool", l)], ["pext"])
            for cb in range(4):
                w3, key = wtile_cols(wv, OP + cb * 256)
                for sub in range(2):
                    blk = cb * 2 + sub
                    b = gbank()
                    proj_fm(w3, sub, key, N, b)
                    act(e4[:, blk, :, 15:15 + L], banks[b][:, :N].rearrange("p (s t) -> p s t", t=L), AF.Copy,
                        [PS(b)], ["pext"])
            if sample:
                for s in range(SS):
                    for hf in range(2):
                        st(o_psm[l, s].rearrange("(b p) t -> p b t", p=128)[:, hf * 4:hf * 4 + 4, :],
                           e4[:, hf * 4:hf * 4 + 4, s, L:L + 15], ["pext"], arena=True)
            elif last:
                for hf in range(2):
                    st(o_pp[l].rearrange("(b p) t -> p b t", p=128)[:, hf * 4:hf * 4 + 4, :],
                       e4[:, hf * 4:hf * 4 + 4, 0, L:L + 15], ["pext"], arena=True)
            else:
                vcopy(st_pool[:, l, :, :], e4[:, :, 0, L:L + 15], ["pext"], [("st_pool", l)])
            for blk in range(8):
                g = blk // 2
                w = POOL_W[g]
                cur = e4[:, blk, :, :]
                curkey = "pext"
                sh = 1
                bufs = [(ta3, "pta"), (tb3, "ptb")]
                bi = 0
                lo = 0
                while sh < w:
                    nxt, nkey = bufs[bi]
                    bi ^= 1
                    lo2 = lo + sh
                    tt(nxt[:, :, lo2:EW], cur[:, :, lo2:EW], cur[:, :, lo2 - sh:EW - sh], ALU.add, [curkey], [nkey])
                    cur, curkey, lo = nxt, nkey, lo2
                    sh *= 2
                zv = zp[:, blk, :N].rearrange("p (s t) -> p s t", t=L)
                stt(zv, cur[:, :, 15:15 + L], 1.0 / w, e4[:, blk, :, 15:15 + L], ALU.mult, ALU.subtract,
                    [curkey, "pext"], [("pzp", blk)])
                if first and not sample:
                    tt(tfix[:, 0:16], cur[:, 0, 15:31], t_icnt[:, g * 16:(g + 1) * 16], ALU.mult, [curkey], ["ptfix"])
                    tt(zp[:, blk, 0:16], tfix[:, 0:16], e4[:, blk, 0, 15:31], ALU.subtract, ["ptfix", "pext"], [("pzp", blk)])
            s_ = W.i % NSLOT
            wp4 = wslots[s_][:, 0:2048].rearrange("p (g c d) -> p g c d", g=4, c=2)
            _, wpkey = W.get([(wp4, w_pool[l].rearrange("g (c p) d -> p g c d", p=128))])
            for g in range(4):
                for dsub in range(2):
                    blk = g * 2 + dsub
                    b = gbank()
                    for cc in range(2):
                        mm(banks[b][:, :N], wp4[:, g, cc, dsub * 128:(dsub + 1) * 128], zp[:, g * 2 + cc, :N],
                           cc == 0, cc == 1, [wpkey, ("pzp", g * 2 + cc)], [PS(b)])
                    ts(brT[:, blk, :N], banks[b][:, :N], cvec(8, l, blk), None, ALU.mult, None, [PS(b)], ["brT"])
        sc.fence()

    def retention(l, N, sample, first, last):
        L = ST if sample else 128
        C_idx = 1 if sample else 0
        wv = w_in[l].rearrange("(kc p) n -> p kc n", p=128)
        with ExitStack() as ph:
            rqT = sb("r_q", [128, 4, N], BF16, ph)
            rkT = sb("r_k", [128, 4, N + 32], BF16, ph)
            Vr = sb("r_V", [128, 4, 1024], BF16, ph)
            kdk = sb("r_kd", [128, 4, 512], BF16, ph)
            rgs = sb("r_g", [128, 2, N], BF16, ph)
            scb = [sb(f"r_sc{i}", [128, 128], BF16, ph) for i in range(2)]
            qd = sb("r_qd", [128, 128], BF16, ph)
            osb = sb("r_o", [128, N], F32, ph)
            obf = sb("r_obf", [128, N], BF16, ph)
            osq = sb("r_osq", [128, N], BF16, ph)
            mean = sb("r_mean", [128, N], F32, ph)
            rstd = sb("r_rstd", [128, N], F32, ph)
            Ssb = sb("r_S", [128, SS, 4, 128], F32, ph) if sample else None
            t_dec = sb("t_dec", [128, 8 * 128], F32, ph)
            t_gpow = sb("t_gpow", [128, 4 * 128], F32, ph)
            t_kdec = sb("t_kdec", [128, 2 * 512], F32, ph)
            for t_, nm in ((t_dec, "tab_dec"), (t_gpow, "tab_gpow"), (t_kdec, "tab_kdec")):
                ld(t_[:], tin[nm][:, :], [nm], arena=True)
            if sample:
                sc.op(DVE, lambda: nc.vector.memset(Vr[:].rearrange("p a b -> p (a b)"), 0.0), (), ["rV"])
                sc.op(DVE, lambda: nc.vector.memset(kdk[:].rearrange("p a b -> p (a b)"), 0.0), (), ["rkd"])
                sc.op(DVE, lambda: nc.vector.memset(rkT[:].rearrange("p a b -> p (a b)"), 0.0), (), ["rk"])
                for i2 in range(2):
                    sc.op(DVE, lambda: nc.vector.memset(scb[i2][:], 0.0), (), [("rsc", i2)])
            for cb in range(2):
                w3, key = wtile_cols(wv, ORQ + cb * 256)
                for sub in range(2):
                    b = gbank()
                    proj_fm(w3, sub, key, N, b)
                    act(rqT[:, cb * 2 + sub, :N], banks[b][:, :N], AF.Copy, [PS(b)], ["rq"])
            kt = []
            for cb in range(2):
                w3, key = wtile_cols(wv, ORK + cb * 256)
                kt.append((w3, key))
                for sub in range(2):
                    b = gbank()
                    proj_fm(w3, sub, key, N, b)
                    act(rkT[:, cb * 2 + sub, :N], banks[b][:, :N], AF.Copy, [PS(b)], ["rk"])
                for blk in range(4):
                    c0 = blk * L
                    b = gbank()
                    LM = max(L, 32)
                    for kc in range(KC):
                        mm(banks[b][:LM, 0:256], hT[:, kc, c0:c0 + LM], w3[:, kc, :], kc == 0, kc == KC - 1,
                           [key, "hT"], [PS(b)])
                    tt(kdk[:LM, blk, cb * 256:(cb + 1) * 256], banks[b][:LM, 0:256],
                       t_kdec[:LM, C_idx * 512 + cb * 256:C_idx * 512 + (cb + 1) * 256], ALU.mult, [PS(b), "tab_kdec"], ["rkd"])
            for cb in range(4):
                w3, key = wtile_cols(wv, ORV + cb * 256)
                for blk in range(4):
                    c0 = blk * L
                    b = gbank()
                    LM = max(L, 32)
                    for kc in range(KC):
                        mm(banks[b][:LM, 0:256], hT[:, kc, c0:c0 + LM], w3[:, kc, :], kc == 0, kc == KC - 1,
                           [key, "hT"], [PS(b)])
                    if sample:
                        vcopy(Vr[:LM, blk, cb * 256:(cb + 1) * 256], banks[b][:LM, 0:256], [PS(b)], ["rV"])
                    else:
                        act(Vr[:LM, blk, cb * 256:(cb + 1) * 256], banks[b][:LM, 0:256], AF.Copy, [PS(b)], ["rV"])
            if sample:
                for s in range(SS):
                    for half in range(2):
                        ld(Ssb[64 * half:64 * half + 64, s, :, :],
                           strt[l, s].rearrange("(j two) d v -> two d j v", two=2)[half], ["rS"], arena=True)
            elif first:
                sc.op(DVE, lambda: nc.vector.memset(st_S[:, l, :, :], 0.0), (), [("st_S", l)])
            dec3 = t_dec[:].rearrange("p (h i) -> p h i", h=8)
            gp3 = t_gpow[:].rearrange("p (j i) -> p j i", j=4)
            sc.op(DVE, lambda: nc.vector.memset(Spad[:].rearrange("p a b -> p (a b)"), 0.0), (), ["Spad"])
            for j in range(4):
                for blk in range(4):
                    c0 = blk * L

                    def Sap(p0, p1):
                        return Ssb[p0:p1, blk, j, :] if sample else st_S[p0:p1, l, j, :]
                    Skey = "rS" if sample else ("st_S", l)
                    if sample or blk == 0:
                        for half in range(2):
                            vcopy(Spad[64 * half:64 * half + 64, 2 * j + half, :], Sap(64 * half, 64 * half + 64),
                                  [Skey], ["Spad"])
                    tt(qd[:, 0:L], rqT[:, j, c0:c0 + L], gp3[:, j, 0:L], ALU.mult, ["rq", "tab_gpow"], ["rqd"])
                    for half in range(2):
                        h = 2 * j + half
                        p0 = 64 * half
                        bs = 4 + half
                        LM = max(L, 32)
                        mm(banks[bs][:LM, 0:L], rkT[p0:p0 + 64, j, c0:c0 + LM], rqT[p0:p0 + 64, j, c0:c0 + L],
                           True, True, ["rk", "rq"], [PS(bs)])
                        tt(scb[half][:LM, 0:L], banks[bs][:LM, 0:L], dec3[:LM, h, 0:L], ALU.mult, [PS(bs), "tab_dec"], [("rsc", half)])
                        bo = 6 + half
                        mm(banks[bo][:, c0:c0 + L], Vr[:, blk, h * 128:(h + 1) * 128], scb[half][:, 0:L],
                           True, False, ["rV", ("rsc", half)], [PS(bo)], signal=False)
                        mm(banks[bo][:, c0:c0 + L], Spad[:, h, :], qd[:, 0:L], False, True, ["Spad", "rqd"], [PS(bo)])
                    b = gbank()
                    mm(banks[b][:, 0:256], kdk[:, blk, j * 128:(j + 1) * 128], Vr[:, blk, j * 256:(j + 1) * 256],
                       True, True, ["rkd", "rV"], [PS(b)])
                    for half in range(2):
                        p0 = 64 * half
                        stt(Sap(p0, p0 + 64), Sap(p0, p0 + 64), t_gc[p0:p0 + 64, C_idx * 4 + j:C_idx * 4 + j + 1],
                            banks[b][p0:p0 + 64, half * 128:(half + 1) * 128], ALU.mult, ALU.add,
                            [Skey, PS(b), "Spad"], [Skey])
                        if (not sample) and blk < 3:
                            vcopy(Spad[p0:p0 + 64, 2 * j + half, :], Sap(p0, p0 + 64), [Skey], ["Spad"])
                w3, key = wtile_cols(wv, ORG + j * 256)
                for sub in range(2):
                    b = gbank()
                    proj_fm(w3, sub, key, N, b)
                    act(rgs[:, sub, :N], banks[b][:, :N], AF.Silu, [PS(b)], ["rgs"])
                for half in range(2):
                    h = 2 * j + half
                    bo = 6 + half
                    act(osb[:, :N], banks[bo][:, :N], AF.Copy, [PS(bo)], ["ro"])
                    act(obf[:, :N], banks[bo][:, :N], AF.Copy, [PS(bo)], ["robf"])
                    act(osq[:, :N], banks[bo][:, :N], AF.Square, [PS(bo)], ["rosq"])
                    bm, bq = gbank(), gbank()
                    mm(banks[bm][:, :N], mats[:, 3, :], obf[:, :N], True, True, ["robf"], [PS(bm)])
                    mm(banks[bq][:, :N], mats[:, 3, :], osq[:, :N], True, True, ["rosq"], [PS(bq)])
                    act(mean[:, :N], banks[bm][:, :N], AF.Copy, [PS(bm)], ["rmean"])
                    tt(rstd[:, :N], mean[:, :N], mean[:, :N], ALU.mult, ["rmean"], ["rrstd"])
                    tt(rstd[:, :N], banks[bq][:, :N], rstd[:, :N], ALU.subtract, [PS(bq), "rrstd"], ["rrstd"])
                    ts(rstd[:, :N], rstd[:, :N], 0.0, None, ALU.max, None, ["rrstd"], ["rrstd"])
                    act(rstd[:, :N], rstd[:, :N], AF.Sqrt, ["rrstd"], ["rrstd"], bias=epsb[:, 0:1])
                    recip(rstd[:, :N], rstd[:, :N], ["rrstd"], ["rrstd"])
                    tt(osb[:, :N], osb[:, :N], mean[:, :N], ALU.subtract, ["ro", "rmean"], ["ro"])
                    tt(osb[:, :N], osb[:, :N], rstd[:, :N], ALU.mult, ["ro", "rrstd"], ["ro"])
                    stt(brT[:, h, :N], osb[:, :N], cvec(9, l, h), rgs[:, half, :N], ALU.mult, ALU.mult,
                        ["ro", "rgs"], ["brT"])
            if sample:
                for s in range(SS):
                    for half in range(2):
                        st(o_rs[l, s].rearrange("(j two) d v -> two d j v", two=2)[half],
                           Ssb[64 * half:64 * half + 64, s, :, :], ["rS"], arena=True)
            elif last:
                for half in range(2):
                    st(o_rp[l].rearrange("(j two) d v -> two d j v", two=2)[half],
                       st_S[64 * half:64 * half + 64, l, :, :], [("st_S", l)])
        sc.fence()

    def out_and_mlp(l, N, sample):
        wo = w_out[l].rearrange("(kc p) n -> p kc n", p=128)
        w1 = w_mlp1[l].rearrange("(kc p) n -> p kc n", p=128)
        w2 = w_mlp2[l].rearrange("(kc p) n -> p kc n", p=128)
        for kc in range(KC):
            act(hT[:, kc, :N], merged[:, kc, :N], AF.Copy, [("merged", kc)], ["hT"])

        def resid(bank, db, gidx):
            for (c0, n, col) in colgroups(sample):
                stt(xr[:, db, c0:c0 + n], banks[bank][:, c0:c0 + n], ada4[:, l, gidx, db, col:col + 1],
                    xr[:, db, c0:c0 + n], ALU.mult, ALU.add, [PS(bank), "xr"], ["xr"])
        for cb in range(8):
            w3, key = wtile_cols(wo, cb * 256)
            for sub in range(2):
                b = gbank()
                proj_fm(w3, sub, key, N, b)
                resid(b, cb * 2 + sub, 2)
        with ExitStack() as ph:
            rmsnorm_mod(l, N, sample, A2v, 3, ph)
            actb = [sb(f"m_act{i}", [128, 8, N], BF16, ph) for i in range(2)]
            rl = [sb(f"m_rl{i}", [128, N], F32, ph) for i in range(2)]
            for jc in range(8):
                ab = actb[jc % 2]
                akey = ("mact", jc % 2)
                for cb in range(4):
                    w3, key = wtile_cols(w1, jc * 1024 + cb * 256)
                    for sub in range(2):
                        hb = cb * 2 + sub
                        i2 = hb % 2
                        b = gbank()
                        proj_fm(w3, sub, key, N, b)
                        act(rl[i2][:, :N], banks[b][:, :N], AF.Relu, [PS(b)], [("mrl", i2)])
                        tt(ab[:, hb, :N], rl[i2][:, :N], rl[i2][:, :N], ALU.mult, [("mrl", i2)], [akey])
                for dq in range(4):
                    s = W.i % NSLOT
                    dst = wslots[s][:, 0:8 * 512].rearrange("p (k c) -> p k c", c=512)
                    _, key = W.get([(dst, w2[:, jc * 8:(jc + 1) * 8, dq * 512:(dq + 1) * 512])])
                    for sub in range(4):
                        db = dq * 4 + sub
                        b = gbank()
                        for kc in range(8):
                            mm(banks[b][:, :N], dst[:, kc, sub * 128:(sub + 1) * 128], ab[:, kc, :N],
                               kc == 0, kc == 7, [key, akey], [PS(b)])
                        resid(b, db, 5)
        sc.fence()

    def layer(l, N, sample, first, last, pi):
        with ExitStack() as ph:
            rmsnorm_mod(l, N, sample, A1v, 0, ph)
        sc.fence()
        chk("norm")
        attention(l, N, sample, first, last, pi)
        chk("attn")
        merge_branch(l, 0, N)
        chk("merge0")
        conv_module(l, N, sample, first, last)
        chk("conv")
        merge_branch(l, 1, N)
        pool_mixer(l, N, sample, first, last)
        chk("pool")
        merge_branch(l, 2, N)
        retention(l, N, sample, first, last)
        chk("ret")
        merge_branch(l, 3, N)
        out_and_mlp(l, N, sample)
        chk("mlp")

    try:
        chk("setup")
        for pi in range(npass):
            sc.new_epoch()
            ld(xr[:, :, :], xT.rearrange("(kc p) t -> p kc t", p=128)[:, :, pi * NPT:(pi + 1) * NPT], ["xr"])
            chk("xload")
            for l in range(nlayers):
                layer(l, NPT, False, pi == 0, pi == npass - 1, pi)
            st(o_yT.rearrange("(kc p) t -> p kc t", p=128)[:, :, pi * NPT:(pi + 1) * NPT], xr[:, :, :], ["xr"])
        if do_sample:
            sc.new_epoch()
            ld(xr[:, :, 0:NST], xsT.rearrange("(kc p) t -> p kc t", p=128), ["xr"])
            for l in range(nlayers):
                layer(l, NST, True, False, False, -1)
            st(o_ysT.rearrange("(kc p) t -> p kc t", p=128), xr[:, :, 0:NST], ["xr"])
    except StopBuild:
        sc.fence()
        st(o_yT.rearrange("(kc p) t -> p kc t", p=128)[:, :, 0:NPT], xr[:, :, :], ["xr"])
        st(o_ysT.rearrange("(kc p) t -> p kc t", p=128), hT[:, :, 0:NST], ["hT"]) if False else None
    nc.sync.wait_ge(outch.sem, outch.cnt)
    W.flush()
    build_program.es = es
    return nc


_CACHE = {}


def _get_program(cfg):
    if cfg not in _CACHE:
        _CACHE[cfg] = build_program(*cfg)
    return _CACHE[cfg]


def kernel(x_prompt, x_sample, c_prompt, c_sample, cache_attn_k, cache_attn_v, state_conv, state_pool, state_ret,
           w_ada, b_ada, g_norm1, g_norm2, w_in, g_qnorm, g_knorm, attn_sinks, w_dw, b_dw, g_conv_ln, b_conv_ln,
           w_pool, s_pool, g_ret_norm, w_br, w_out, w_mlp1, w_mlp2, _cfg=None, _ncores=8):
    f = lambda a: np.ascontiguousarray(np.asarray(a, dtype=np.float32))
    cfg = _cfg or (DEPTH, NPASS, True, DEPTH)
    ncores = _ncores
    wd = cfg[3]
    nc = _get_program(cfg)
    tabs = host_tables()
    shared = dict(w_ada=f(w_ada[:wd]), b_ada=f(b_ada), g_norm1=f(g_norm1), g_norm2=f(g_norm2), w_in=f(w_in[:wd]),
                  g_qnorm=f(g_qnorm), g_knorm=f(g_knorm), attn_sinks=f(attn_sinks), w_dw=f(w_dw), b_dw=f(b_dw),
                  g_conv_ln=f(g_conv_ln), b_conv_ln=f(b_conv_ln), w_pool=f(w_pool), s_pool=f(s_pool),
                  g_ret_norm=f(g_ret_norm), w_br=f(w_br[:wd]), w_out=f(w_out[:wd]), w_mlp1=f(w_mlp1[:wd]),
                  w_mlp2=f(w_mlp2[:wd]))
    shared.update(tabs)
    x_prompt, x_sample = f(x_prompt), f(x_sample)
    c_prompt, c_sample = f(c_prompt), f(c_sample)
    ck, cv = f(cache_attn_k), f(cache_attn_v)
    stc, stp, strt = f(state_conv), f(state_pool), f(state_ret)
    xTs = [np.ascontiguousarray(x_prompt[b].T) for b in range(2)]
    in_maps = []
    for c in range(ncores):
        sl = slice(SS * c, SS * c + SS)
        cs = np.concatenate([c_prompt[c % 2][None], c_sample[sl]], 0)
        cTl = np.ascontiguousarray(cs.reshape(5, KC, 128).transpose(2, 1, 0).reshape(128, KC * 5))
        m = dict(shared)
        m.update(xT=xTs[c % 2],
                 xsT=np.ascontiguousarray(x_sample[sl].reshape(NST, D).T),
                 cT=cTl,
                 ck=np.ascontiguousarray(ck[:, sl].reshape(DEPTH, SS, 128, 256)),
                 cv=np.ascontiguousarray(cv[:, sl].reshape(DEPTH, SS, 128, 256)),
                 stc=np.ascontiguousarray(stc[:, sl]), stp=np.ascontiguousarray(stp[:, sl]),
                 strt=np.ascontiguousarray(strt[:, sl]))
        in_maps.append(m)
    res = run_bass_kernel_spmd(nc, in_maps, core_ids=list(range(ncores)))
    R = res.results
    if ncores < 8:
        return R
    y_prompt = np.stack([R[b]["o_yT"].T for b in range(2)], 0)
    y_sample = np.concatenate([R[c]["o_ysT"].T.reshape(SS, ST, D) for c in range(8)], 0)
    nk_p = np.stack([R[b]["o_kp"].transpose(0, 3, 1, 2) for b in range(2)], 1)
    nv_p = np.stack([R[b]["o_vp"].reshape(DEPTH, 128, 4, 64) for b in range(2)], 1)
    nc_p = np.stack([R[b]["o_cp"].transpose(0, 2, 1) for b in range(2)], 1)
    np_p = np.stack([R[b]["o_pp"].transpose(0, 2, 1) for b in range(2)], 1)
    nr_p = np.stack([R[b]["o_rp"] for b in range(2)], 1)
    nk_s = np.concatenate([R[c]["o_ks"].reshape(DEPTH, SS, 128, 4, 64) for c in range(8)], 1)
    nv_s = np.concatenate([R[c]["o_vs"].reshape(DEPTH, SS, 128, 4, 64) for c in range(8)], 1)
    nc_s = np.concatenate([R[c]["o_cs"].transpose(0, 1, 3, 2) for c in range(8)], 1)
    np_s = np.concatenate([R[c]["o_psm"].transpose(0, 1, 3, 2) for c in range(8)], 1)
    nr_s = np.concatenate([R[c]["o_rs"] for c in range(8)], 1)
    outs = (y_prompt, y_sample, nk_p, nv_p, nc_p, np_p, nr_p, nk_s, nv_s, nc_s, np_s, nr_s)
    return tuple(np.ascontiguousarray(o.astype(np.float32)) for o in outs)
```
